# Optimizing a Trainium2 kernel written in Bass

```python
import math
import jax, jax.numpy as jnp
from jax import lax
import numpy as np

D_MODEL = 1024
BATCH = 1
SEQ = 16384
DEPTH = 2

N_A_LAYERS = DEPTH // 2
N_B_LAYERS = DEPTH - N_A_LAYERS
PLE_DIM = 256
DN_ALPHA = (2.0 * DEPTH) ** 0.25
DN_BETA = (8.0 * DEPTH) ** -0.25
NORM_EPS = 1e-5

ML_HEADS = 8
ML_QK_DIM = D_MODEL // (2 * ML_HEADS)
ML_V_DIM = D_MODEL // ML_HEADS
ML_CHUNK = 64
ML_Q_END = ML_HEADS * ML_QK_DIM
ML_K_END = 2 * ML_HEADS * ML_QK_DIM
ML_V_END = ML_K_END + ML_HEADS * ML_V_DIM
ML_I_END = ML_V_END + ML_HEADS
ML_F_END = ML_I_END + ML_HEADS
ML_IN_COLS = ML_F_END + D_MODEL

MLA_HEADS = 8
MLA_NOPE_DIM = 128
MLA_ROPE_DIM = 64
MLA_V_DIM = 128
MLA_KV_RANK = 256
MLA_Q_RANK = 384
ROPE_THETA = 10000.0
ATTN_BLOCK = 128

PEER_HEADS = 8
PEER_N_KEYS = 128
PEER_N_EXPERTS = PEER_N_KEYS * PEER_N_KEYS
PEER_KEY_DIM = 256
PEER_HALF = PEER_KEY_DIM // 2
PEER_TOPK = 16
PEER_BLOCK = 128

kernel_name = "yoco_mlstm_mla_peer_deepnorm"


def layer_norm(x, g, b):
    xf = x.astype(jnp.float32)
    mu = jnp.mean(xf, -1, keepdims=True)
    var = jnp.mean(jnp.square(xf - mu), -1, keepdims=True)
    y = (xf - mu) * lax.rsqrt(var + NORM_EPS)
    return (y * g.astype(jnp.float32) + b.astype(jnp.float32)).astype(x.dtype)


def rms_norm(x, g):
    xf = x.astype(jnp.float32)
    y = xf * lax.rsqrt(jnp.mean(jnp.square(xf), -1, keepdims=True) + NORM_EPS)
    return (y * g.astype(jnp.float32)).astype(x.dtype)


def rope_angles(positions, dim):
    inv_freq = ROPE_THETA ** (-jnp.arange(0, dim, 2, dtype=jnp.float32) / dim)
    ang = positions.astype(jnp.float32)[..., None] * inv_freq
    return jnp.cos(ang), jnp.sin(ang)


def apply_rope(x, cos, sin):
    xf = x.astype(jnp.float32)
    x1, x2 = jnp.split(xf, 2, axis=-1)
    return jnp.concatenate([x1 * cos - x2 * sin, x1 * sin + x2 * cos], -1).astype(x.dtype)


def mlstm_chunk_step(carry, inp):
    C, n, m = carry
    q, k, v, ig, lf = inp
    L = q.shape[2]
    b = jnp.cumsum(lf, axis=-1)
    causal = jnp.tril(jnp.ones((L, L), dtype=bool))
    log_w = jnp.where(causal, b[..., :, None] - b[..., None, :] + ig[..., None, :], -jnp.inf)
    log_inter = b + m[..., None]
    m_t = jnp.maximum(log_inter, jnp.max(log_w, -1))
    w = jnp.exp(log_w - m_t[..., None])
    s_inter = jnp.exp(log_inter - m_t)
    qk = jnp.einsum('bhtd,bhsd->bhts', q, k) * w
    num = jnp.einsum('bhts,bhsv->bhtv', qk, v) + s_inter[..., None] * jnp.einsum('bhtd,bhdv->bhtv', q, C)
    den = jnp.sum(qk, -1) + s_inter * jnp.einsum('bhtd,bhd->bht', q, n)
    h = num / jnp.maximum(jnp.abs(den), jnp.exp(-m_t))[..., None]
    b_last = b[..., -1]
    log_src = b_last[..., None] - b + ig
    m_new = jnp.maximum(b_last + m, jnp.max(log_src, -1))
    w_src = jnp.exp(log_src - m_new[..., None])
    s_old = jnp.exp(b_last + m - m_new)
    C_new = s_old[..., None, None] * C + jnp.einsum('bhs,bhsd,bhsv->bhdv', w_src, k, v)
    n_new = s_old[..., None] * n + jnp.einsum('bhs,bhsd->bhd', w_src, k)
    return (C_new, n_new, m_new), h


def mlstm_mixer(x, w_in, b_if, hn_g, w_out):
    B, S, _ = x.shape
    H, L = ML_HEADS, ML_CHUNK
    nc = S // L
    proj = x @ w_in
    q, k, v = proj[..., :ML_Q_END], proj[..., ML_Q_END:ML_K_END], proj[..., ML_K_END:ML_V_END]
    gi, gf, og = proj[..., ML_V_END:ML_I_END], proj[..., ML_I_END:ML_F_END], proj[..., ML_F_END:]

    def to_chunks(t, d):
        return t.reshape(B, nc, L, H, d).transpose(1, 0, 3, 2, 4).astype(jnp.float32)

    def gate_chunks(t):
        return t.reshape(B, nc, L, H).transpose(1, 0, 3, 2)

    qc = to_chunks(q, ML_QK_DIM)
    kc = to_chunks(k, ML_QK_DIM) * (ML_QK_DIM ** -0.5)
    vc = to_chunks(v, ML_V_DIM)
    b_if = b_if.astype(jnp.float32)
    ig = gate_chunks(gi.astype(jnp.float32) + b_if[0])
    lf = gate_chunks(jax.nn.log_sigmoid(gf.astype(jnp.float32) + b_if[1]))
    init = (jnp.zeros((B, H, ML_QK_DIM, ML_V_DIM), jnp.float32),
            jnp.zeros((B, H, ML_QK_DIM), jnp.float32),
            jnp.zeros((B, H), jnp.float32))
    _, h = lax.scan(mlstm_chunk_step, init, (qc, kc, vc, ig, lf))
    h = h.transpose(1, 0, 3, 2, 4).reshape(B, S, H, ML_V_DIM)
    mu = jnp.mean(h, -1, keepdims=True)
    var = jnp.mean(jnp.square(h - mu), -1, keepdims=True)
    hn = (h - mu) * lax.rsqrt(var + NORM_EPS) * hn_g.astype(jnp.float32).reshape(H, ML_V_DIM)
    out = jax.nn.sigmoid(og.astype(jnp.float32)) * hn.reshape(B, S, H * ML_V_DIM)
    return out.astype(x.dtype) @ w_out


def mla_shared_kv(xs, w_down, kv_norm_g, w_up, cos, sin):
    B, S, _ = xs.shape
    ckr = xs @ w_down
    c_kv = rms_norm(ckr[..., :MLA_KV_RANK], kv_norm_g)
    kv = (c_kv @ w_up).reshape(B, S, MLA_HEADS, MLA_NOPE_DIM + MLA_V_DIM)
    k_nope = kv[..., :MLA_NOPE_DIM].transpose(0, 2, 1, 3)
    v = kv[..., MLA_NOPE_DIM:].transpose(0, 2, 1, 3)
    k_rope = apply_rope(ckr[..., MLA_KV_RANK:], cos, sin)
    return k_nope, k_rope, v


def mla_mixer(x, w_dq, q_norm_g, w_uq, w_out, k_nope, k_rope, v, cos, sin):
    B, S, _ = x.shape
    H = MLA_HEADS
    c_q = rms_norm(x @ w_dq, q_norm_g)
    q = (c_q @ w_uq).reshape(B, S, H, MLA_NOPE_DIM + MLA_ROPE_DIM).transpose(0, 2, 1, 3)
    q_nope = q[..., :MLA_NOPE_DIM]
    q_rope = apply_rope(q[..., MLA_NOPE_DIM:], cos[:, None], sin[:, None])
    nb = S // ATTN_BLOCK

    def blocks(t):
        return t.reshape(B, H, nb, ATTN_BLOCK, t.shape[-1]).transpose(2, 0, 1, 3, 4)

    scale = (MLA_NOPE_DIM + MLA_ROPE_DIM) ** -0.5
    key_pos = jnp.arange(S)

    def attend(args):
        idx, qn, qr = args
        s = jnp.einsum('bhqd,bhkd->bhqk', qn, k_nope) + jnp.einsum('bhqd,bkd->bhqk', qr, k_rope)
        s = s.astype(jnp.float32) * scale
        q_pos = idx * ATTN_BLOCK + jnp.arange(ATTN_BLOCK)
        s = jnp.where(key_pos[None, :] <= q_pos[:, None], s, -jnp.inf)
        p = jax.nn.softmax(s, axis=-1).astype(v.dtype)
        return jnp.einsum('bhqk,bhkd->bhqd', p, v)

    o = lax.map(attend, (jnp.arange(nb), blocks(q_nope), blocks(q_rope)))
    o = o.transpose(1, 0, 3, 2, 4).reshape(B, S, H * MLA_V_DIM)
    return o @ w_out


def peer_ffn(x, w_q, sub_keys, u, v):
    B, S, D = x.shape
    T = B * S
    K = PEER_TOPK
    xt = x.reshape(T, D)
    q = (xt @ w_q).reshape(T, PEER_HEADS, 2, PEER_HALF)
    s = jnp.einsum('thcd,hcnd->thcn', q, sub_keys).astype(jnp.float32)
    s1, i1 = lax.top_k(s[:, :, 0], K)
    s2, i2 = lax.top_k(s[:, :, 1], K)
    cand_s = (s1[..., :, None] + s2[..., None, :]).reshape(T, PEER_HEADS, K * K)
    cand_i = (i1[..., :, None] * PEER_N_KEYS + i2[..., None, :]).reshape(T, PEER_HEADS, K * K)
    top_s, top_pos = lax.top_k(cand_s, K)
    expert_idx = jnp.take_along_axis(cand_i, top_pos, axis=-1)
    gates = jax.nn.softmax(top_s, axis=-1).astype(x.dtype)
    nb = T // PEER_BLOCK

    def expert_block(args):
        xb, eb, gb = args
        ub = jnp.take(u, eb, axis=0)
        act = jax.nn.gelu(jnp.einsum('thkd,td->thk', ub, xb), approximate=False)
        vb = jnp.take(v, eb, axis=0)
        return jnp.einsum('thk,thkd->td', gb * act, vb)

    out = lax.map(expert_block, (xt.reshape(nb, PEER_BLOCK, D),
                                 expert_idx.reshape(nb, PEER_BLOCK, PEER_HEADS, K),
                                 gates.reshape(nb, PEER_BLOCK, PEER_HEADS, K)))
    return out.reshape(B, S, D)


def setup_inputs(seed: int = 0) -> dict:
    key = jax.random.key(seed)
    ks = jax.random.split(key, 24)
    D = D_MODEL
    nrm = jax.random.normal
    f32 = jnp.float32
    x = nrm(ks[0], (BATCH, SEQ, D), f32)
    p = nrm(ks[1], (DEPTH, BATCH, SEQ, PLE_DIM), f32)
    positions = jnp.broadcast_to(jnp.arange(SEQ, dtype=jnp.int32), (BATCH, SEQ))
    ln_g = 1.0 + 0.05 * nrm(ks[2], (DEPTH, 2, D), f32)
    ln_b = 0.02 * nrm(ks[3], (DEPTH, 2, D), f32)
    a_w_in = nrm(ks[4], (N_A_LAYERS, D, ML_IN_COLS), f32) * D ** -0.5
    b_i = 0.1 * nrm(ks[5], (N_A_LAYERS, ML_HEADS), f32)
    b_f = jnp.linspace(3.0, 6.0, ML_HEADS, dtype=f32) + 0.1 * nrm(ks[6], (N_A_LAYERS, ML_HEADS), f32)
    a_b_if = jnp.stack([b_i, b_f], axis=1)
    a_hn_g = 1.0 + 0.05 * nrm(ks[7], (N_A_LAYERS, D), f32)
    a_w_out = nrm(ks[8], (N_A_LAYERS, D, D), f32) * (D ** -0.5 * DN_BETA)
    kv_w_down = nrm(ks[9], (D, MLA_KV_RANK + MLA_ROPE_DIM), f32) * D ** -0.5
    kv_norm_g = 1.0 + 0.05 * nrm(ks[10], (MLA_KV_RANK,), f32)
    kv_w_up = nrm(ks[11], (MLA_KV_RANK, MLA_HEADS * (MLA_NOPE_DIM + MLA_V_DIM)), f32) * MLA_KV_RANK ** -0.5
    b_w_dq = nrm(ks[12], (N_B_LAYERS, D, MLA_Q_RANK), f32) * D ** -0.5
    b_q_norm_g = 1.0 + 0.05 * nrm(ks[13], (N_B_LAYERS, MLA_Q_RANK), f32)
    b_w_uq = nrm(ks[14], (N_B_LAYERS, MLA_Q_RANK, MLA_HEADS * (MLA_NOPE_DIM + MLA_ROPE_DIM)), f32) * MLA_Q_RANK ** -0.5
    b_w_out = nrm(ks[15], (N_B_LAYERS, MLA_HEADS * MLA_V_DIM, D), f32) * ((MLA_HEADS * MLA_V_DIM) ** -0.5 * DN_BETA)
    peer_w_q = nrm(ks[16], (DEPTH, D, PEER_HEADS * PEER_KEY_DIM), f32) * D ** -0.5
    peer_sub_keys = nrm(ks[17], (DEPTH, PEER_HEADS, 2, PEER_N_KEYS, PEER_HALF), f32) * PEER_HALF ** -0.5
    peer_u = nrm(ks[18], (DEPTH, PEER_N_EXPERTS, D), f32) * D ** -0.5
    peer_v = nrm(ks[19], (DEPTH, PEER_N_EXPERTS, D), f32) * (DN_BETA * PEER_HEADS ** -0.5)
    ple_w_proj = nrm(ks[20], (DEPTH, PLE_DIM, D), f32) * (0.5 * PLE_DIM ** -0.5)
    ple_w_gate = nrm(ks[21], (DEPTH, D, D), f32) * D ** -0.5
    return {"x": x, "p": p, "positions": positions, "ln_g": ln_g, "ln_b": ln_b,
            "a_w_in": a_w_in, "a_b_if": a_b_if, "a_hn_g": a_hn_g, "a_w_out": a_w_out,
            "kv_w_down": kv_w_down, "kv_norm_g": kv_norm_g, "kv_w_up": kv_w_up,
            "b_w_dq": b_w_dq, "b_q_norm_g": b_q_norm_g, "b_w_uq": b_w_uq, "b_w_out": b_w_out,
            "peer_w_q": peer_w_q, "peer_sub_keys": peer_sub_keys, "peer_u": peer_u, "peer_v": peer_v,
            "ple_w_proj": ple_w_proj, "ple_w_gate": ple_w_gate}


def reference(x, p, positions, ln_g, ln_b, a_w_in, a_b_if, a_hn_g, a_w_out,
              kv_w_down, kv_norm_g, kv_w_up, b_w_dq, b_q_norm_g, b_w_uq, b_w_out,
              peer_w_q, peer_sub_keys, peer_u, peer_v, ple_w_proj, ple_w_gate):
    cos, sin = rope_angles(positions, MLA_ROPE_DIM)
    shared = None
    for i in range(DEPTH):
        if i < N_A_LAYERS:
            mix = mlstm_mixer(x, a_w_in[i], a_b_if[i], a_hn_g[i], a_w_out[i])
        else:
            j = i - N_A_LAYERS
            if j == 0:
                shared = mla_shared_kv(x, kv_w_down, kv_norm_g, kv_w_up, cos, sin)
            k_nope, k_rope, v = shared
            mix = mla_mixer(x, b_w_dq[j], b_q_norm_g[j], b_w_uq[j], b_w_out[j],
                            k_nope, k_rope, v, cos, sin)
        x = layer_norm(DN_ALPHA * x + mix, ln_g[i, 0], ln_b[i, 0])
        x = layer_norm(DN_ALPHA * x + peer_ffn(x, peer_w_q[i], peer_sub_keys[i], peer_u[i], peer_v[i]),
                       ln_g[i, 1], ln_b[i, 1])
        x = x + jax.nn.sigmoid(x @ ple_w_gate[i]) * (p[i] @ ple_w_proj[i])
    return x
```

```python
import numpy as np
import concourse.bass as bass
import concourse.mybir as mybir
from concourse.bass_utils import run_bass_kernel_spmd
from contextlib import ExitStack

F32 = mybir.dt.float32
BF16 = mybir.dt.bfloat16
I32 = mybir.dt.int32
U32 = mybir.dt.uint32
ALU = mybir.AluOpType
AF = mybir.ActivationFunctionType
AX = mybir.AxisListType

import os as _os
EPOCH = int(_os.environ.get("K_EPOCH", 12000))
SLOT_EPOCH = int(_os.environ.get("K_SLOT_EPOCH", 9600))
NCORES = 8
SEQ = 16384
D = 1024
ALPHA = 4.0 ** 0.25
EPS = 1e-5


class Sched:
    ARENA_WORDS = 52000

    def __init__(self, nc, stack):
        self.nc = nc
        self.stack = stack
        self.ops = []
        self.incs = []
        self.barriers = []
        self.nsem = 0
        self.limit = None
        self.arena = stack.enter_context(nc.sbuf_tensor("arena", [128, self.ARENA_WORDS], F32))
        self.banks = [stack.enter_context(nc.psum_tensor(f"bank{k}", [128, 512], F32)) for k in range(8)]
        self.off = 0
        self.nbank = 0

    def phase(self):
        self.off = 0
        self.nbank = 0
        if self.ops:
            self.barriers.append(len(self.ops))

    def _newsem(self, name):
        self.nsem += 1
        return self.stack.enter_context(self.nc.semaphore(f"{name}_{self.nsem}"))

    def sb(self, name, shape, dt=F32):
        shape = list(shape)
        esz = 2 if dt == BF16 else 4
        n = 1
        for d_ in shape[1:]:
            n *= d_
        words = (n * esz + 3) // 4
        words += words % 2
        assert self.off + words <= self.ARENA_WORDS, (name, self.off, words)
        v = self.arena[0:shape[0], self.off:self.off + words]
        self.off += words
        if dt != F32:
            v = v.bitcast(dt)
        v = v[:, 0:n]
        if len(shape) == 3:
            v = v.rearrange("p (a b) -> p a b", a=shape[1])
        elif len(shape) == 4:
            v = v.rearrange("p (a b c) -> p a b c", a=shape[1], b=shape[2])
        return v

    def ps(self, name, shape, dt=F32):
        b = self.banks[self.nbank]
        self.nbank += 1
        return b

    def op(self, eng, fn, r=(), w=(), slot=None, inc=16):
        if self.limit is not None and len(self.ops) >= self.limit and not (slot or "").startswith("s_"):
            return
        self.ops.append((eng, fn, tuple(r), tuple(w), slot))
        self.incs.append(inc)

    def mm(self, out, lhsT, rhs, start=True, stop=True, r=(), w=()):
        self.op("pe", lambda e: e.matmul(out, lhsT=lhsT, rhs=rhs, start=start, stop=stop), r, w)

    def tr(self, out, in_, ident, r=(), w=()):
        self.op("pe", lambda e: e.transpose(out, in_, ident), r, w)

    def act(self, out, in_, func, bias=0.0, scale=1.0, accum_out=None, r=(), w=()):
        if accum_out is None:
            self.op("act", lambda e: e.activation(out=out, in_=in_, func=func, bias=bias, scale=scale), r, w)
        else:
            self.op("act", lambda e: e.activation(out=out, in_=in_, func=func, bias=bias, scale=scale,
                                                  accum_out=accum_out), r, w)

    def cp(self, eng, out, in_, r=(), w=()):
        if eng == "act":
            self.op("act", lambda e: e.copy(out=out, in_=in_), r, w)
        else:
            self.op(eng, lambda e: e.tensor_copy(out=out, in_=in_), r, w)

    def tt(self, eng, out, in0, in1, op, r=(), w=()):
        self.op(eng, lambda e: e.tensor_tensor(out=out, in0=in0, in1=in1, op=op), r, w)

    def ts(self, eng, out, in0, s1, s2=None, op0=ALU.mult, op1=None, accum_out=None, r=(), w=()):
        kw = {}
        if op1 is not None:
            kw["op1"] = op1
        if accum_out is not None:
            kw["accum_out"] = accum_out
        self.op(eng, lambda e: e.tensor_scalar(out=out, in0=in0, scalar1=s1, scalar2=s2, op0=op0, **kw), r, w)

    def stt(self, out, in0, scalar, in1, op0, op1, accum_out=None, r=(), w=()):
        if accum_out is None:
            self.op("dve", lambda e: e.scalar_tensor_tensor(out=out, in0=in0, scalar=scalar, in1=in1,
                                                            op0=op0, op1=op1), r, w)
        else:
            self.op("dve", lambda e: e.scalar_tensor_tensor(out=out, in0=in0, scalar=scalar, in1=in1,
                                                            op0=op0, op1=op1, accum_out=accum_out), r, w)

    def recip(self, out, in_, r=(), w=()):
        self.op("dve", lambda e: e.reciprocal(out=out, in_=in_), r, w)

    def memset(self, eng, ap, val, w=()):
        self.op(eng, lambda e: e.memset(ap, val), (), w)

    def dma(self, eng, out, in_, r=(), w=(), slot=None):
        assert slot is not None
        self.op(eng, lambda e: e.dma_start(out=out, in_=in_), r, w, slot)

    def gather(self, out, table, idx_ap, r=(), w=(), slot=None):
        self.op("pool", lambda e: e.indirect_dma_start(
            out=out, out_offset=None, in_=table,
            in_offset=bass.IndirectOffsetOnAxis(ap=idx_ap, axis=0)), r, w, slot)

    @staticmethod
    def _flat(ap, rows):
        R, C = ap.shape
        if R == rows:
            return ap
        if R > rows:
            return ap.rearrange("(p k) c -> p (k c)", k=R // rows)
        return ap.rearrange("r (j c) -> (r j) c", j=rows // R)

    def cc(self, kind, in_ap, out_ap, r=(), w=(), slot=None):
        in_ap = self._flat(in_ap, 256)
        out_ap = self._flat(out_ap, 256 * NCORES)
        self.op("pool", lambda e: e.collective_compute(kind, ALU.bypass, replica_groups=[list(range(NCORES))],
                                                       ins=[in_ap], outs=[out_ap]), r, w, slot, inc=1)

    def emit(self):
        nc = self.nc
        ops = self.ops
        n = len(ops)
        last_writer = {}
        readers = {}
        last_slot = {}
        bank_last = {}
        deps = [None] * n
        last_eng = {}
        fence = set()
        barriers = set(self.barriers)
        for i, (eng, fn, rd, wr, slot) in enumerate(ops):
            if i in barriers:
                fence = set(last_eng.values()) | set(last_slot.values())
            d = set(fence)
            if slot is None:
                last_eng[eng] = i
            for b in rd:
                if b in last_writer:
                    d.add(last_writer[b])
            for b in wr:
                if b in last_writer:
                    d.add(last_writer[b])
                for x in readers.get(b, ()):
                    d.add(x)
            if slot is not None:
                if slot in last_slot:
                    d.add(last_slot[slot])
                last_slot[slot] = i
            for b in set(rd) | set(wr):
                if isinstance(b, tuple) and b[0] == "B":
                    la = bank_last.setdefault(b, {})
                    for e2, j2 in la.items():
                        if e2 != eng:
                            d.add(j2)
                    la[eng] = i
            d.discard(i)
            for b in rd:
                readers.setdefault(b, []).append(i)
            for b in wr:
                last_writer[b] = i
                readers[b] = []
            keep = {}
            for j in d:
                ej, _, _, _, sj = ops[j]
                if sj is not None:
                    key = ("slot", sj)
                else:
                    if ej == "pe" and eng == "pe" and slot is None:
                        continue
                    key = ("eng", ej)
                if key not in keep or keep[key] < j:
                    keep[key] = j
            deps[i] = sorted(keep.values())
        signal = [False] * n
        for i in range(n):
            for j in deps[i]:
                signal[j] = True
        semof = [None] * n
        valof = [0] * n
        eng_cnt = {}
        eng_sems = {}
        slot_cnt = {}
        slot_sem = {}
        final_extra = []
        for i, (eng, fn, rd, wr, slot) in enumerate(ops):
            if slot is not None:
                if slot not in slot_sem or slot_cnt[slot] + self.incs[i] > SLOT_EPOCH:
                    if slot in slot_sem:
                        final_extra.append((slot_sem[slot], slot_cnt[slot]))
                    slot_sem[slot] = self._newsem("d")
                    slot_cnt[slot] = 0
                slot_cnt[slot] += self.incs[i]
                semof[i] = slot_sem[slot]
                valof[i] = slot_cnt[slot]
            elif signal[i]:
                c = eng_cnt.get(eng, 0)
                ep = c // EPOCH
                lst = eng_sems.setdefault(eng, [])
                if len(lst) <= ep:
                    lst.append(self._newsem(eng))
                semof[i] = lst[ep]
                valof[i] = c % EPOCH + 1
                eng_cnt[eng] = c + 1
        streams = {}
        waited = {}
        for i, (eng, fn, rd, wr, slot) in enumerate(ops):
            st = streams.setdefault(eng, [])
            wt = waited.setdefault(eng, {})
            for j in deps[i]:
                s = semof[j]
                v = valof[j]
                k = id(s)
                if wt.get(k, 0) < v:
                    wt[k] = v
                    st.append(("w", s, v))
            st.append(("o", fn, semof[i], (self.incs[i] if slot is not None else 1)))
        final = [(slot_sem[s], slot_cnt[s]) for s in slot_sem] + final_extra
        print('sched: ops', n, 'sems', self.nsem)
        engmap = {"pe": "tensor", "act": "scalar", "dve": "vector", "pool": "gpsimd", "sp": "sync"}
        with nc.Block() as block:
            for eng, st in streams.items():
                def body(e, st=st, eng=eng):
                    for item in st:
                        if item[0] == "w":
                            e.wait_ge(item[1], item[2])
                        else:
                            ins = item[1](e)
                            if item[2] is not None:
                                ins.then_inc(item[2], item[3])
                    if eng == "sp":
                        for s, v in final:
                            e.wait_ge(s, v)
                getattr(block, engmap[eng])(body)
            if "sp" not in streams:
                def body(e):
                    for s, v in final:
                        e.wait_ge(s, v)
                block.sync(body)
        return n


def din(nc, name, shape, dt=F32):
    return nc.dram_tensor(name, list(shape), dt, kind="ExternalInput").ap()


def dout(nc, name, shape, dt=F32):
    return nc.dram_tensor(name, list(shape), dt, kind="ExternalOutput").ap()


M_CST = 128 + 128 + 2 + 64


def m_consts():
    s = np.arange(128)
    same = (s[:, None] // 64) == (s[None, :] // 64)
    mask = (same & (s[:, None] <= s[None, :])).astype(np.float32)
    blk = same.astype(np.float32)
    sel = np.stack([(s // 64 == 0), (s // 64 == 1)], 1).astype(np.float32)
    ones = np.ones((128, 64), np.float32)
    return np.concatenate([mask, blk, sel, ones], 1)


def phase_M(s, S, t, after_fence=None):
    xT, wqk, wtm, bif, hng, cst, ident, gT = (t[k] for k in ("xT", "wqk", "wtm", "bif", "hng", "cst_m", "ident", "gT"))
    NB = S // 512
    s.phase()
    if after_fence is not None:
        after_fence()
    if True:
        wqk_sb = s.sb("wqk", [128, 8, 128])
        wtm_sb = s.sb("wtm", [128, 8, 322])
        bif_sb = s.sb("bif", [128, 2])
        nbf = s.sb("nbf", [128, 1])
        hng_sb = s.sb("hng", [128, 128])
        cst_sb = s.sb("cst", [128, M_CST])
        id_sb = s.sb("ident", [128, 128])
        mask = cst_sb[:, 0:128]
        blk = cst_sb[:, 128:256]
        sel = cst_sb[:, 256:258]
        ones64 = cst_sb[:, 258:322]
        s.dma("sp", wqk_sb[:], wqk.rearrange("(c p) m -> p c m", p=128), w=["wqk"], slot="l_wqk")
        s.dma("sp", wtm_sb[:], wtm.rearrange("(c p) m -> p c m", p=128), w=["wtm"], slot="l_wtm")
        s.dma("sp", bif_sb[:], bif, w=["bif"], slot="l_bif")
        s.dma("sp", hng_sb[:], hng, w=["hng"], slot="l_hng")
        s.dma("sp", cst_sb[:], cst, w=["cst"], slot="l_cst")
        s.dma("sp", id_sb[:], ident, w=["ident"], slot="l_id")
        s.ts("dve", nbf[:], bif_sb[:, 1:2], -1.0, r=["bif"], w=["nbf"])
        eps_sb = s.sb("eps", [128, 1])
        s.memset("dve", eps_sb[:], EPS, w=["eps"])

        xTb = [s.sb(f"xTb{j}", [128, 8, 512]) for j in range(2)]
        qTs = [s.sb(f"qT{j}", [64, 512]) for j in range(2)]
        kTs = [s.sb(f"kT{j}", [64, 512]) for j in range(2)]
        gTb = [s.sb(f"gTb{j}", [128, 512]) for j in range(2)]
        v1 = [s.sb(f"v1_{j}", [128, 129]) for j in range(2)]
        ktm = [s.sb(f"ktm{j}", [128, 64]) for j in range(2)]
        eog = [s.sb(f"eog{j}", [128, 128]) for j in range(2)]
        g = [s.sb(f"g{j}", [128, 16]) for j in range(2)]
        nlf2 = [s.sb(f"nlf2{j}", [128, 2]) for j in range(2)]
        nbs = [s.sb(f"nbs{j}", [128, 2]) for j in range(2)]
        ec = [s.sb(f"ec{j}", [64, 2]) for j in range(2)]
        AT = [s.sb(f"AT{j}", [128, 128]) for j in range(2)]
        vp = [s.sb(f"vp{j}", [128, 129]) for j in range(2)]
        vpp = [s.sb(f"vpp{j}", [128, 129]) for j in range(2)]
        vppb = [s.sb(f"vppb{j}", [128, 129]) for j in range(2)]
        hh = [s.sb(f"hh{j}", [128, 128]) for j in range(2)]
        hn = [s.sb(f"hn{j}", [128, 128]) for j in range(2)]
        sg = [s.sb(f"sg{j}", [128, 128]) for j in range(2)]
        bst = [s.sb(f"bst{j}", [128, 6]) for j in range(2)]
        mv = [s.sb(f"mv{j}", [128, 2]) for j in range(2)]
        NST = 4
        Cn = [s.sb(f"Cn{j}", [64, 129]) for j in range(NST)]
        for j in range(2):
            s.memset("dve", v1[j][:, 128:129], 1.0, w=[("v1", j)])
        s.memset("dve", Cn[0][:], 0.0, w=[("Cn", 0)])

        qp = s.ps("qp", [128, 512])
        kp = s.ps("kp", [128, 512])
        tmb = [s.ps(f"tm{j}", [128, 512]) for j in range(2)]
        Hb = [s.ps(f"Hb{j}", [128, 512]) for j in range(2)]
        sb4 = [s.ps(f"sb4{j}", [128, 512]) for j in range(2)]
        BQ, BK = ("B", "q"), ("B", "k")
        BT = [("B", "t0"), ("B", "t1")]
        BH = [("B", "h0"), ("B", "h1")]
        BS = [("B", "s0"), ("B", "s1")]
        tm = [tmb[j][:, 0:322] for j in range(2)]
        gp = [tmb[j][:, 400:402] for j in range(2)]
        ecp = [tmb[j][0:64, 410:412] for j in range(2)]
        Hp = [Hb[j][:, 0:129] for j in range(2)]
        Pa = [Hb[j][0:64, 200:329] for j in range(2)]
        Pb = [Hb[j][0:64, 340:469] for j in range(2)]
        sc = [sb4[j][:, 0:128] for j in range(2)]
        tp = [sb4[j][:, 128:256] for j in range(2)]

        xTv = xT.rearrange("(c p) t -> p c t", p=128)
        chunk = 0
        for b in range(NB):
            bj = b % 2
            s.dma("sp", xTb[bj][:], xTv[:, :, b * 512:(b + 1) * 512], w=[("xTb", bj)], slot=f"l_x{bj}")
            for c in range(8):
                s.mm(qp[0:64, :], wqk_sb[:, c, 0:64], xTb[bj][:, c, :], start=(c == 0), stop=(c == 7),
                     r=["wqk", ("xTb", bj), BQ], w=["qp"])
            for c in range(8):
                s.mm(kp[0:64, :], wqk_sb[:, c, 64:128], xTb[bj][:, c, :], start=(c == 0), stop=(c == 7),
                     r=["wqk", ("xTb", bj), BK], w=["kp"])
            s.cp("act", qTs[bj][:], qp[0:64, :], r=["qp", BQ], w=[("qT", bj)])
            s.act(kTs[bj][:], kp[0:64, :], AF.Copy, scale=0.125, r=["kp", BK], w=[("kT", bj)])
            for ti in range(4):
                i = b * 4 + ti
                j = i % 2
                cols = slice(ti * 128, (ti + 1) * 128)
                ca = chunk % NST
                cb = (chunk + 1) % NST
                cc = (chunk + 2) % NST
                chunk += 2
                for c in range(8):
                    s.mm(tm[j], xTb[bj][:, c, cols], wtm_sb[:, c, :], start=(c == 0), stop=(c == 7),
                         r=[("xTb", bj), "wtm", BT[j]], w=[("tm", j)])
                s.cp("act", v1[j][:, 0:128], tm[j][:, 0:128], r=[("tm", j), BT[j]], w=[("v1", j)])
                s.act(ktm[j][:], tm[j][:, 256:320], AF.Copy, scale=0.125, r=[("tm", j), BT[j]], w=[("ktm", j)])
                s.act(eog[j][:], tm[j][:, 128:256], AF.Exp, scale=-1.0, r=[("tm", j), BT[j]], w=[("eog", j)])
                s.act(g[j][:, 0:1], tm[j][:, 321:322], AF.Exp, bias=nbf[:, 0:1], scale=-1.0,
                      r=[("tm", j), "nbf", BT[j]], w=[("g0", j)])
                s.act(g[j][:, 2:3], tm[j][:, 320:321], AF.Identity, bias=bif_sb[:, 0:1],
                      r=[("tm", j), "bif", BT[j]], w=[("g2", j)])
                s.act(g[j][:, 1:2], g[j][:, 0:1], AF.Ln, bias=1.0, r=[("g0", j)], w=[("g1", j)])
                s.ts("dve", nlf2[j][:], sel, g[j][:, 1:2], op0=ALU.mult, r=["cst", ("g1", j)], w=[("nlf2", j)])
                s.mm(gp[j][:, 0:1], mask, g[j][:, 1:2], r=["cst", ("g1", j), BT[j]], w=[("gp", j)])
                s.mm(gp[j][:, 1:2], blk, g[j][:, 1:2], r=["cst", ("g1", j), BT[j]], w=[("gp", j)])
                s.mm(ecp[j], ones64, nlf2[j][:], r=["cst", ("nlf2", j), BT[j]], w=[("ecp", j)])
                s.cp("act", nbs[j][:], gp[j], r=[("gp", j), BT[j]], w=[("nbs", j)])
                s.act(ec[j][:], ecp[j], AF.Exp, scale=-1.0, r=[("ecp", j), BT[j]], w=[("ec", j)])
                s.tt("dve", g[j][:, 3:4], nbs[j][:, 0:1], nbs[j][:, 1:2], ALU.subtract, r=[("nbs", j)], w=[("g3", j)])
                s.act(g[j][:, 4:5], nbs[j][:, 0:1], AF.Exp, scale=-1.0, r=[("nbs", j)], w=[("g4", j)])
                s.act(g[j][:, 5:6], nbs[j][:, 0:1], AF.Exp, bias=g[j][:, 2:3], r=[("nbs", j), ("g2", j)], w=[("g5", j)])
                s.act(g[j][:, 6:7], g[j][:, 3:4], AF.Exp, bias=g[j][:, 2:3], r=[("g3", j), ("g2", j)], w=[("g6", j)])
                s.ts("dve", vp[j][:], v1[j][:], g[j][:, 5:6], op0=ALU.mult, r=[("v1", j), ("g5", j)], w=[("vp", j)])
                s.ts("dve", g[j][:, 14:16], sel, g[j][:, 6:7], op0=ALU.mult, r=["cst", ("g6", j)], w=[("g14", j)])
                s.ts("dve", vpp[j][:], v1[j][:], g[j][:, 14:15], op0=ALU.mult, r=[("v1", j), ("g14", j)], w=[("vpp", j)])
                s.ts("dve", vppb[j][:], v1[j][:], g[j][:, 15:16], op0=ALU.mult, r=[("v1", j), ("g14", j)], w=[("vppb", j)])
                s.mm(sc[j], kTs[bj][:, cols], qTs[bj][:, cols], r=[("kT", bj), ("qT", bj), BS[j]], w=[("sc", j)])
                s.tt("dve", AT[j][:], sc[j], mask, ALU.mult, r=[("sc", j), "cst", BS[j]], w=[("AT", j)])
                s.mm(Pa[j], ktm[j][:], vpp[j][:], r=[("ktm", j), ("vpp", j), BH[j]], w=[("Pa", j)])
                s.mm(Pb[j], ktm[j][:], vppb[j][:], r=[("ktm", j), ("vppb", j), BH[j]], w=[("Pb", j)])
                s.stt(Cn[cb][:], Cn[ca][:], ec[j][:, 0:1], Pa[j], ALU.mult, ALU.add,
                      r=[("Cn", ca), ("ec", j), ("Pa", j), BH[j]], w=[("Cn", cb)])
                s.stt(Cn[cc][:], Cn[cb][:], ec[j][:, 1:2], Pb[j], ALU.mult, ALU.add,
                      r=[("Cn", cb), ("ec", j), ("Pb", j), BH[j]], w=[("Cn", cc)])
                s.mm(Hp[j], AT[j][:], vp[j][:], start=True, stop=False, r=[("AT", j), ("vp", j), BH[j]], w=[("Hp", j)])
                s.mm(Hp[j][0:64, :], qTs[bj][:, ti * 128:ti * 128 + 64], Cn[ca][:], start=False, stop=False,
                     r=[("qT", bj), ("Cn", ca), BH[j]], w=[("Hp", j)])
                s.mm(Hp[j][64:128, :], qTs[bj][:, ti * 128 + 64:ti * 128 + 128], Cn[cb][:], start=False, stop=True,
                     r=[("qT", bj), ("Cn", cb), BH[j]], w=[("Hp", j)])
                s.ts("dve", g[j][:, 11:12], Hp[j][:, 128:129], g[j][:, 4:5], op0=ALU.mult,
                     r=[("Hp", j), ("g4", j), BH[j]], w=[("g11", j)])
                s.stt(g[j][:, 12:13], g[j][:, 11:12], -1.0, g[j][:, 11:12], ALU.mult, ALU.max,
                      r=[("g11", j)], w=[("g12", j)])
                s.ts("dve", g[j][:, 7:8], g[j][:, 12:13], 1.0, op0=ALU.max, r=[("g12", j)], w=[("g7", j)])
                s.recip(g[j][:, 8:9], g[j][:, 7:8], r=[("g7", j)], w=[("g8", j)])
                s.tt("dve", g[j][:, 9:10], g[j][:, 8:9], g[j][:, 4:5], ALU.mult, r=[("g8", j), ("g4", j)], w=[("g9", j)])
                s.ts("dve", hh[j][:], Hp[j][:, 0:128], g[j][:, 9:10], op0=ALU.mult,
                     r=[("Hp", j), ("g9", j), BH[j]], w=[("hh", j)])
                s.op("dve", lambda e, j=j: e.bn_stats(out=bst[j][:], in_=hh[j][:]), r=[("hh", j)], w=[("bst", j)])
                s.op("dve", lambda e, j=j: e.bn_aggr(out=mv[j][:], in_=bst[j][:]), r=[("bst", j)], w=[("mv", j)])
                s.act(g[j][:, 13:14], mv[j][:, 1:2], AF.Ln, bias=eps_sb[:, 0:1], r=[("mv", j), "eps"], w=[("g13", j)])
                s.act(g[j][:, 10:11], g[j][:, 13:14], AF.Exp, scale=-0.5, r=[("g13", j)], w=[("g10", j)])
                s.ts("dve", hn[j][:], hh[j][:], mv[j][:, 0:1], g[j][:, 10:11], op0=ALU.subtract, op1=ALU.mult,
                     r=[("hh", j), ("mv", j), ("g10", j)], w=[("hn", j)])
                s.tt("dve", hn[j][:], hn[j][:], hng_sb[:], ALU.mult, r=[("hn", j), "hng"], w=[("hn", j)])
                s.ts("dve", sg[j][:], eog[j][:], 1.0, op0=ALU.add, r=[("eog", j)], w=[("sg", j)])
                s.recip(sg[j][:], sg[j][:], r=[("sg", j)], w=[("sg", j)])
                s.tt("dve", hn[j][:], hn[j][:], sg[j][:], ALU.mult, r=[("hn", j), ("sg", j)], w=[("hn", j)])
                s.tr(tp[j], hn[j][:], id_sb[:], r=[("hn", j), "ident", BS[j]], w=[("tp", j)])
                s.cp("act", gTb[bj][:, cols], tp[j], r=[("tp", j), BS[j]], w=[("gTb", bj)])
            s.dma("sp", gT[:, b * 512:(b + 1) * 512], gTb[bj][:], r=[("gTb", bj)], slot=f"s_g{bj}")
    return


def build_M(S):
    nc = bass.Bass("TRN2", target_bir_lowering=False)
    t = dict(xT=din(nc, "xT", [D, S]), wqk=din(nc, "wqk", [D, 128]), wtm=din(nc, "wtm", [D, 322]),
             bif=din(nc, "bif", [128, 2]), hng=din(nc, "hng", [128, 128]), cst_m=din(nc, "cst", [128, M_CST]),
             ident=din(nc, "ident", [128, 128]), gT=dout(nc, "gT", [128, S]))
    with ExitStack() as st:
        s = Sched(nc, st)
        phase_M(s, S, t)
        n = s.emit()
    return nc, n


C1_2PI = 6.28125
C2_2PI = 2.0 * np.pi - 6.28125


def t_consts():
    ident = np.eye(128, dtype=np.float32)
    iota = np.broadcast_to(np.arange(256, dtype=np.float32)[None, :], (128, 256))
    invf = (10000.0 ** (-np.arange(0, 64, 2, dtype=np.float32) / 64)).astype(np.float32)
    invf = np.broadcast_to(invf[None, :], (128, 32))
    return np.ascontiguousarray(np.concatenate([ident, iota, invf], 1))


def t_decl(nc, NT, kind, sfx="", tables=True):
    NTOK = NT * 128
    t = dict(
        w_out=din(nc, "w_out" + sfx, [D, D]), lnp=din(nc, "lnp" + sfx, [128, 4, D]),
        w_q=din(nc, "w_q" + sfx, [D, 2048]), skT=din(nc, "skT" + sfx, [128, 16, 128]),
        pu=din(nc, "peer_u" + sfx, [16384, D]) if tables else None,
        pv=din(nc, "peer_v" + sfx, [16384, D]) if tables else None,
        pT=din(nc, "pT" + sfx, [256, NTOK]), w_proj=din(nc, "w_proj" + sfx, [256, D]),
        w_gate=din(nc, "w_gate" + sfx, [D, D]))
    if kind == "A":
        t.update(w_dn=din(nc, "w_dn", [D, 704]), w_up=din(nc, "w_up", [256, 2048]), w_uq=din(nc, "w_uq", [384, 1536]),
                 gn=din(nc, "gn", [128, 640]), pos=din(nc, "pos", [NTOK, 1], I32))
    return t


def phase_T(s, NT, kind, t, limit_slots=128):
    nc = s.nc
    NTOK = NT * 128
    xin, w_out, lnp, w_q, skT, pu, pv, pT, w_proj, w_gate, cst, o_x2 = (
        t[k] for k in ("xin", "w_out", "lnp", "w_q", "skT", "pu", "pv", "pT", "w_proj", "w_gate", "cst_t", "o_x2"))
    mT = t.get("mT")
    g_rows = t.get("g_rows")
    idxm = t.get("idxm")
    if kind == "A":
        w_dn, w_up, w_uq, gn, pos, o_qn, o_qr, o_kn, o_kr, o_v = (
            t[k] for k in ("w_dn", "w_up", "w_uq", "gn", "pos", "o_qn", "o_qr", "o_kn", "o_kr", "o_v"))
    s.phase()
    if True:
        cst_sb = s.sb("cst", [128, 416])
        ident = cst_sb[:, 0:128]
        iota = cst_sb[:, 128:384]
        invf = cst_sb[:, 384:416]
        lnp_sb = s.sb("lnp", [128, 4, D])
        skT_sb = s.sb("skT", [128, 16, 128])
        eps_sb = s.sb("eps", [128, 1])
        s.dma("sp", cst_sb[:], cst, w=["cst"], slot="l_cst")
        s.dma("sp", lnp_sb[:], lnp, w=["lnp"], slot="l_lnp")
        s.dma("sp", skT_sb[:], skT, w=["skT"], slot="l_skT")
        s.memset("dve", eps_sb[:], EPS, w=["eps"])
        if kind == "A":
            gn_sb = s.sb("gn", [128, 640])
            s.dma("sp", gn_sb[:], gn, w=["gn"], slot="l_gn")
        if g_rows is not None:
            idxm_sb = s.sb("idxm", [128, NT * 8], I32)
            s.dma("sp", idxm_sb[:], idxm, w=["idxm"], slot="l_idxm")
        NW = 2
        wb = [s.sb(f"wb{k}", [128, 8, 1024]) for k in range(NW)]
        wcnt = [0]

        def wload(src_ap, shape_fn):
            k = wcnt[0] % NW
            wcnt[0] += 1
            view = shape_fn(wb[k])
            s.dma("sp", view, src_ap, w=[("wb", k)], slot=f"l_wb{k}")
            return view, ("wb", k)

        NG = 4
        ubuf = [s.sb(f"gb{k}", [128, D]) for k in range(NG)]
        vbuf = ubuf
        xin_sb = s.sb("xin", [128, D])
        mT_sb = s.sb("mTt", [128, 8, 128])
        y = s.sb("y", [128, D])
        x1 = s.sb("x1", [128, D])
        xT = s.sb("xT", [128, 8, 128])
        junk = s.sb("junk", [128, D])
        sco = s.sb("sco", [128, 2048])
        qTs = s.sb("qTs", [128, 512])
        top = s.sb("top", [128, 16, 16])
        ti = s.sb("ti", [128, 16, 16], U32)
        tif = s.sb("tif", [128, 16, 16])
        scr = s.sb("scr", [128, 256])
        cand = s.sb("cand", [128, 8, 256])
        candi = s.sb("candi", [128, 8, 256])
        ctop = s.sb("ctop", [128, 8, 16])
        cpos = s.sb("cpos", [128, 8, 16], U32)
        cposf = s.sb("cposf", [128, 128])
        ef = s.sb("ef", [128, 128])
        eidx = s.sb("eidx", [128, 128], I32)
        gate = s.sb("gate", [128, 8, 16])
        gz = s.sb("gz", [128, 16])
        dots = s.sb("dots", [128, 128])
        ga = s.sb("ga", [128, 128])
        dg = [s.sb(f"dg{k}", [128, 128]) for k in range(2)]
        x2p = s.sb("x2p", [128, D])
        sig = s.sb("sig", [128, D])
        pT_sb = s.sb("pTt", [128, 2, 128])
        bst = s.sb("bst", [128, 2, 6])
        mv = s.sb("mv", [128, 2])
        sm = s.sb("sm", [128, 8])

        if kind == "A":
            ckv = s.sb("ckv", [128, 256]); cq = s.sb("cq", [128, 384])
            krr = s.sb("krr", [128, 64]); kro = s.sb("kro", [128, 64]); krT_sb = s.sb("krT", [64, 128])
            posi = s.sb("posi", [128, 1], I32); rp = s.sb("rp", [128, 2])
            ang = s.sb("ang", [128, 32]); kf = s.sb("kf", [128, 32]); ki = s.sb("ki", [128, 32], I32)
            r1 = s.sb("r1", [128, 32]); sn = s.sb("sn", [128, 32]); cs = s.sb("cs", [128, 32]); t32 = s.sb("t32", [128, 32])
            ckvT = s.sb("ckvT", [128, 2, 128]); cqT = s.sb("cqT", [128, 3, 128])
            big = [s.sb(f"big{k}", [128, 8, 128]) for k in range(2)]
            qro = s.sb("qro", [128, 512]); t256 = s.sb("t256", [128, 256])
        A = [s.ps(f"A{k}", [128, 512]) for k in range(2)]
        Tb = [s.ps(f"Tb{k}", [128, 512]) for k in range(2)]
        Sb = [s.ps(f"Sb{k}", [128, 512]) for k in range(2)]
        BA = [("B", "A0"), ("B", "A1")]
        BTb = [("B", "T0"), ("B", "T1")]
        BSb = [("B", "S0"), ("B", "S1")]

        def layer_norm(src, dst, gi):
            for k in range(2):
                s.op("dve", lambda e, k=k: e.bn_stats(out=bst[:, k, :], in_=src[:, k * 512:(k + 1) * 512]),
                     r=[src_key[0]], w=[("bst", k)])
            s.op("dve", lambda e: e.bn_aggr(out=mv[:], in_=bst[:].rearrange("p a b -> p (a b)")),
                 r=[("bst", 0), ("bst", 1)], w=["mv"])
            s.act(sm[:, 0:1], mv[:, 1:2], AF.Ln, bias=eps_sb[:, 0:1], r=["mv", "eps"], w=["sm0"])
            s.act(sm[:, 1:2], sm[:, 0:1], AF.Exp, scale=-0.5, r=["sm0"], w=["sm1"])
            s.ts("dve", dst[:], src[:], mv[:, 0:1], sm[:, 1:2], op0=ALU.subtract, op1=ALU.mult,
                 r=[src_key[0], "mv", "sm1"], w=[dst_key[0]])
            s.tt("dve", dst[:], dst[:], lnp_sb[:, 2 * gi, :], ALU.mult, r=[dst_key[0], "lnp"], w=[dst_key[0]])
            s.tt("dve", dst[:], dst[:], lnp_sb[:, 2 * gi + 1, :], ALU.add, r=[dst_key[0], "lnp"], w=[dst_key[0]])

        src_key = [None]
        dst_key = [None]

        def transpose8(src, skey, dst, dkey):
            for half in range(2):
                tb = Tb[half]
                for c in range(4):
                    cc = half * 4 + c
                    s.tr(tb[:, c * 128:(c + 1) * 128], src[:, cc * 128:(cc + 1) * 128], ident,
                         r=[skey, "cst", BTb[half]], w=[("tb", half)])
                s.cp("act", dst[:, half * 4:(half + 1) * 4, :], tb[:].rearrange("p (c t) -> p c t", c=4),
                     r=[("tb", half), BTb[half]], w=[dkey])

        for t in range(NT):
            tok = slice(t * 128, (t + 1) * 128)
            s.dma("sp", xin_sb[:], xin[tok, :], w=["xin"], slot="l_xin")
            if g_rows is None:
                s.dma("sp", mT_sb[:], mT.rearrange("(c p) t -> p c t", p=128)[:, :, tok], w=["mTt"], slot="l_mT")
            else:
                for hh_ in range(8):
                    s.gather(mT_sb[:, hh_, :], g_rows, idxm_sb[:, t * 8 + hh_:t * 8 + hh_ + 1], r=["idxm"], w=["mTt"],
                             slot=f"g_m{hh_}")
            wv, wk = wload(w_out.rearrange("(c p) n -> p c n", p=128), lambda b: b[:])
            for half in range(2):
                for c in range(8):
                    s.mm(A[half][:], mT_sb[:, c, :], wv[:, c, half * 512:(half + 1) * 512], start=(c == 0), stop=(c == 7),
                         r=["mTt", wk, BA[half]], w=[("A", half)])
            for half in range(2):
                hs = slice(half * 512, (half + 1) * 512)
                s.stt(y[:, hs], xin_sb[:, hs], ALPHA, A[half][:], ALU.mult, ALU.add,
                      r=["xin", ("A", half), BA[half]], w=["y"])
            src_key[0], dst_key[0] = "y", "x1"
            layer_norm(y, x1, 0)
            transpose8(x1, "x1", xT, "xT")
            for wh in range(2):
                wv, wk = wload(w_q.rearrange("(c p) n -> p c n", p=128)[:, :, wh * 1024:(wh + 1) * 1024], lambda b: b[:])
                for gg in range(2):
                    tbk = gg
                    for g4 in range(4):
                        col = gg * 512 + g4 * 128
                        for c in range(8):
                            s.mm(Tb[tbk][:, g4 * 128:(g4 + 1) * 128], wv[:, c, col:col + 128], xT[:, c, :],
                                 start=(c == 0), stop=(c == 7), r=[wk, "xT", BTb[tbk]], w=[("tb", tbk)])
                    s.cp("act", qTs[:], Tb[tbk][:], r=[("tb", tbk), BTb[tbk]], w=["qTs"])
                    sbk = gg
                    for g4 in range(4):
                        gq = wh * 8 + gg * 4 + g4
                        s.mm(Sb[sbk][:, g4 * 128:(g4 + 1) * 128], qTs[:, g4 * 128:(g4 + 1) * 128], skT_sb[:, gq, :],
                             r=["qTs", "skT", BSb[sbk]], w=[("sbk", sbk)])
                    g0 = wh * 8 + gg * 4
                    s.cp("act", sco[:, g0 * 128:(g0 + 4) * 128], Sb[sbk][:], r=[("sbk", sbk), BSb[sbk]], w=["sco"])
            for gq in range(16):
                sg_ = sco[:, gq * 128:(gq + 1) * 128]
                s.op("dve", lambda e, gq=gq, sg_=sg_: e.max(out=top[:, gq, 0:8], in_=sg_), r=["sco"], w=["top"])
                s.op("dve", lambda e, gq=gq, sg_=sg_: e.match_replace(out=scr[:, 0:128], in_to_replace=top[:, gq, 0:8],
                                                                     in_values=sg_, imm_value=-1e30),
                     r=["sco", "top"], w=["scr"])
                s.op("dve", lambda e, gq=gq: e.max(out=top[:, gq, 8:16], in_=scr[:, 0:128]), r=["scr"], w=["top"])
                s.op("dve", lambda e, gq=gq, sg_=sg_: e.max_index(out=ti[:, gq, 0:8], in_max=top[:, gq, 0:8], in_values=sg_),
                     r=["sco", "top"], w=["ti"])
                s.op("dve", lambda e, gq=gq, sg_=sg_: e.max_index(out=ti[:, gq, 8:16], in_max=top[:, gq, 8:16], in_values=sg_),
                     r=["sco", "top"], w=["ti"])
            s.cp("dve", tif[:], ti[:], r=["ti"], w=["tif"])
            top4 = top[:].rearrange("p (h c) k -> p h c k", c=2)
            tif4 = tif[:].rearrange("p (h c) k -> p h c k", c=2)
            cand4 = cand[:].rearrange("p h (r c) -> p h r c", c=16)
            candi4 = candi[:].rearrange("p h (r c) -> p h r c", c=16)
            s.tt("dve", cand4, top4[:, :, 0, :].unsqueeze(3).to_broadcast([128, 8, 16, 16]),
                 top4[:, :, 1, :].unsqueeze(2).to_broadcast([128, 8, 16, 16]), ALU.add, r=["top"], w=["cand"])
            s.ts("dve", tif4[:, :, 0, :], tif4[:, :, 0, :], 128.0, op0=ALU.mult, r=["tif"], w=["tif"])
            s.tt("dve", candi4, tif4[:, :, 0, :].unsqueeze(3).to_broadcast([128, 8, 16, 16]),
                 tif4[:, :, 1, :].unsqueeze(2).to_broadcast([128, 8, 16, 16]), ALU.add, r=["tif"], w=["candi"])
            for h in range(8):
                ch = cand[:, h, :]
                s.op("dve", lambda e, h=h, ch=ch: e.max(out=ctop[:, h, 0:8], in_=ch), r=["cand"], w=["ctop"])
                s.op("dve", lambda e, h=h, ch=ch: e.match_replace(out=scr[:], in_to_replace=ctop[:, h, 0:8], in_values=ch,
                                                                 imm_value=-1e30), r=["cand", "ctop"], w=["scr"])
                s.op("dve", lambda e, h=h: e.max(out=ctop[:, h, 8:16], in_=scr[:]), r=["scr"], w=["ctop"])
                s.op("dve", lambda e, h=h, ch=ch: e.max_index(out=cpos[:, h, 0:8], in_max=ctop[:, h, 0:8], in_values=ch),
                     r=["cand", "ctop"], w=["cpos"])
                s.op("dve", lambda e, h=h, ch=ch: e.max_index(out=cpos[:, h, 8:16], in_max=ctop[:, h, 8:16], in_values=ch),
                     r=["cand", "ctop"], w=["cpos"])
            s.cp("dve", cposf[:], cpos[:].rearrange("p h k -> p (h k)"), r=["cpos"], w=["cposf"])
            s.memset("dve", ef[:], 0.0, w=["ef"])
            for hk in range(128):
                h = hk // 16
                s.stt(junk[:, 0:256], iota, cposf[:, hk:hk + 1], candi[:, h, :], ALU.is_equal, ALU.mult,
                      accum_out=ef[:, hk:hk + 1], r=["cst", "cposf", "candi", "ef"], w=[("ef", hk)])
            s.cp("dve", eidx[:], ef[:], r=[("ef", hk) for hk in range(128)], w=["eidx"])
            s.tt("dve", gate[:], ctop[:], ctop[:, :, 0:1].to_broadcast([128, 8, 16]), ALU.subtract, r=["ctop"], w=["gate"])
            s.act(gate[:], gate[:], AF.Exp, r=["gate"], w=["gate"])
            s.op("dve", lambda e: e.tensor_reduce(out=gz[:, 0:8], in_=gate[:], axis=AX.X, op=ALU.add), r=["gate"], w=["gz"])
            s.recip(gz[:, 8:16], gz[:, 0:8], r=["gz"], w=["gz2"])
            s.tt("dve", gate[:], gate[:], gz[:, 8:16].unsqueeze(2).to_broadcast([128, 8, 16]), ALU.mult,
                 r=["gate", "gz2"], w=["gate"])
            s.memset("dve", dots[:], 0.0, w=["dots"])
            NSL = limit_slots
            for sl in range(NSL):
                k = sl % NG
                s.gather(ubuf[k][:], pu, eidx[:, sl:sl + 1], r=["eidx"], w=[("gb", k)], slot=f"g_u{k}")
                s.stt(junk[:], ubuf[k][:], 1.0, x1[:], ALU.mult, ALU.mult, accum_out=dots[:, sl:sl + 1],
                      r=[("gb", k), "x1", "dots"], w=[("dots", sl)])
            s.act(ga[:], dots[:], AF.Gelu, r=[("dots", sl) for sl in range(NSL)], w=["ga"])
            s.tt("dve", ga[:], ga[:], gate[:].rearrange("p h k -> p (h k)"), ALU.mult, r=["ga", "gate"], w=["ga"])
            for sl in range(NSL):
                k = sl % NG
                dk = sl % 2
                s.gather(vbuf[k][:], pv, eidx[:, sl:sl + 1], r=["eidx"], w=[("gb", k)], slot=f"g_v{k}")
                s.act(dg[dk][:], ident, AF.Copy, scale=ga[:, sl:sl + 1], r=["cst", "ga"], w=[("dg", dk)])
                for half in range(2):
                    s.mm(A[half][:], dg[dk][:], vbuf[k][:, half * 512:(half + 1) * 512], start=(sl == 0), stop=(sl == NSL - 1),
                         r=[("dg", dk), ("gb", k), BA[half]], w=[("A", half)])
            for half in range(2):
                hs = slice(half * 512, (half + 1) * 512)
                s.stt(y[:, hs], x1[:, hs], ALPHA, A[half][:], ALU.mult, ALU.add, r=["x1", ("A", half), BA[half]], w=["y"])
            src_key[0], dst_key[0] = "y", "x2p"
            layer_norm(y, x2p, 1)
            transpose8(x2p, "x2p", xT, "xT")
            s.dma("sp", pT_sb[:], pT.rearrange("(c p) t -> p c t", p=128)[:, :, tok], w=["pTt"], slot="l_pT")
            wv, wk = wload(w_gate.rearrange("(c p) n -> p c n", p=128), lambda b: b[:])
            for half in range(2):
                for c in range(8):
                    s.mm(A[half][:], xT[:, c, :], wv[:, c, half * 512:(half + 1) * 512], start=(c == 0), stop=(c == 7),
                         r=["xT", wk, BA[half]], w=[("A", half)])
            for half in range(2):
                hs = slice(half * 512, (half + 1) * 512)
                s.act(sig[:, hs], A[half][:], AF.Exp, scale=-1.0, r=[("A", half), BA[half]], w=["sig"])
            s.ts("dve", sig[:], sig[:], 1.0, op0=ALU.add, r=["sig"], w=["sig"])
            s.recip(sig[:], sig[:], r=["sig"], w=["sig"])
            wv, wk = wload(w_proj.rearrange("(c p) n -> p c n", p=128), lambda b: b[:, 0:2, :])
            for half in range(2):
                for c in range(2):
                    s.mm(A[half][:], pT_sb[:, c, :], wv[:, c, half * 512:(half + 1) * 512], start=(c == 0), stop=(c == 1),
                         r=["pTt", wk, BA[half]], w=[("A", half)])
            for half in range(2):
                hs = slice(half * 512, (half + 1) * 512)
                s.tt("dve", sig[:, hs], sig[:, hs], A[half][:], ALU.mult, r=["sig", ("A", half), BA[half]], w=["sig"])
            s.tt("dve", x2p[:], x2p[:], sig[:], ALU.add, r=["x2p", "sig"], w=["x2p"])
            s.dma("sp", o_x2[tok, :], x2p[:], r=["x2p"], slot="s_x2")
            if kind == "A":
                transpose8(x2p, "x2p", xT, "xT")
                wv, wk = wload(w_dn.rearrange("(c p) n -> p c n", p=128), lambda b: b[:, :, 0:704])
                for c in range(8):
                    s.mm(A[0][:, 0:320], xT[:, c, :], wv[:, c, 0:320], start=(c == 0), stop=(c == 7),
                         r=["xT", wk, BA[0]], w=[("A", 0)])
                for c in range(8):
                    s.mm(A[1][:, 0:384], xT[:, c, :], wv[:, c, 320:704], start=(c == 0), stop=(c == 7),
                         r=["xT", wk, BA[1]], w=[("A", 1)])
                for (bk, n_, c0, dst, dkey) in ((0, 256, 0, ckv, "ckv"), (1, 384, 256, cq, "cq")):
                    s.act(junk[:, 0:n_], A[bk][:, 0:n_], AF.Square, accum_out=sm[:, 2:3], r=[("A", bk), BA[bk]], w=["sm2"])
                    s.act(sm[:, 3:4], sm[:, 2:3], AF.Ln, scale=1.0 / n_, bias=eps_sb[:, 0:1], r=["sm2", "eps"], w=["sm3"])
                    s.act(sm[:, 4:5], sm[:, 3:4], AF.Exp, scale=-0.5, r=["sm3"], w=["sm4"])
                    s.stt(dst[:], A[bk][:, 0:n_], sm[:, 4:5], gn_sb[:, c0:c0 + n_], ALU.mult, ALU.mult,
                          r=[("A", bk), BA[bk], "sm4", "gn"], w=[dkey])
                s.cp("act", krr[:], A[0][:, 256:320], r=[("A", 0), BA[0]], w=["krr"])
                s.dma("sp", posi[:], pos[tok, :], w=["posi"], slot="l_pos")
                s.cp("dve", rp[:, 0:1], posi[:], r=["posi"], w=["posf"])
                s.ts("dve", ang[:], invf, rp[:, 0:1], op0=ALU.mult, r=["cst", "posf"], w=["ang"])
                s.ts("dve", kf[:], ang[:], float(1.0 / (2.0 * np.pi)), op0=ALU.mult, r=["ang"], w=["kf"])
                s.cp("dve", ki[:], kf[:], r=["kf"], w=["ki"])
                s.cp("dve", kf[:], ki[:], r=["ki"], w=["kf"])
                s.stt(r1[:], kf[:], -C1_2PI, ang[:], ALU.mult, ALU.add, r=["kf", "ang"], w=["r1"])
                s.stt(r1[:], kf[:], -C2_2PI, r1[:], ALU.mult, ALU.add, r=["kf", "r1"], w=["r1"])
                s.ts("dve", r1[:], r1[:], float(np.pi), float(-np.pi), op0=ALU.min, op1=ALU.max, r=["r1"], w=["r1"])
                s.act(sn[:], r1[:], AF.Sin, r=["r1"], w=["sn"])
                s.act(cs[:], r1[:], AF.Sin, scale=0.5, r=["r1"], w=["cs"])
                s.tt("dve", cs[:], cs[:], cs[:], ALU.mult, r=["cs"], w=["cs"])
                s.ts("dve", cs[:], cs[:], -2.0, 1.0, op0=ALU.mult, op1=ALU.add, r=["cs"], w=["cs"])
                s.tt("dve", kro[:, 0:32], krr[:, 0:32], cs[:], ALU.mult, r=["krr", "cs"], w=["kro"])
                s.tt("dve", t32[:], krr[:, 32:64], sn[:], ALU.mult, r=["krr", "sn"], w=["t32"])
                s.tt("dve", kro[:, 0:32], kro[:, 0:32], t32[:], ALU.subtract, r=["kro", "t32"], w=["kro"])
                s.tt("dve", kro[:, 32:64], krr[:, 0:32], sn[:], ALU.mult, r=["krr", "sn", "kro"], w=["kro"])
                s.tt("dve", t32[:], krr[:, 32:64], cs[:], ALU.mult, r=["krr", "cs", "kro"], w=["t32"])
                s.tt("dve", kro[:, 32:64], kro[:, 32:64], t32[:], ALU.add, r=["kro", "t32"], w=["kro"])
                s.tr(Tb[0][0:64, 0:128], kro[:], ident, r=["kro", "cst", BTb[0]], w=[("tb", 0)])
                s.cp("act", krT_sb[:], Tb[0][0:64, 0:128], r=[("tb", 0), BTb[0]], w=["krT"])
                s.dma("sp", o_kr[:, tok], krT_sb[:], r=["krT"], slot="s_kr")
                for c in range(2):
                    s.tr(Tb[0][:, c * 128:(c + 1) * 128], ckv[:, c * 128:(c + 1) * 128], ident,
                         r=["ckv", "cst", BTb[0]], w=[("tb", 0)])
                s.cp("act", ckvT[:], Tb[0][:, 0:256].rearrange("p (c t) -> p c t", c=2), r=[("tb", 0), BTb[0]], w=["ckvT"])
                for c in range(3):
                    s.tr(Tb[1][:, c * 128:(c + 1) * 128], cq[:, c * 128:(c + 1) * 128], ident,
                         r=["cq", "cst", BTb[1]], w=[("tb", 1)])
                s.cp("act", cqT[:], Tb[1][:, 0:384].rearrange("p (c t) -> p c t", c=3), r=[("tb", 1), BTb[1]], w=["cqT"])
                wv, wk = wload(w_up.rearrange("(c p) n -> p c n", p=128),
                               lambda b: b[:].rearrange("p c n -> p (c n)")[:, 0:4096].rearrange("p (c n) -> p c n", c=2))
                for h in range(8):
                    half, h4 = h // 4, h % 4
                    for c in range(2):
                        s.mm(A[half][:, h4 * 128:(h4 + 1) * 128], wv[:, c, h * 256:h * 256 + 128], ckvT[:, c, :],
                             start=(c == 0), stop=(c == 1), r=[wk, "ckvT", BA[half]], w=[("A", half)])
                for half in range(2):
                    s.cp("act", big[0][:, half * 4:(half + 1) * 4, :], A[half][:].rearrange("p (h t) -> p h t", h=4),
                         r=[("A", half), BA[half]], w=[("big", 0)])
                s.dma("sp", o_kn[:, :, tok].rearrange("h d t -> d h t"), big[0][:], r=[("big", 0)], slot="s_kn")
                for half in range(2):
                    for c in range(2):
                        rhs = wv[:, c, :].rearrange("p (h x) -> p h x", x=256)[:, half * 4:(half + 1) * 4, 128:256]
                        s.mm(A[half][:].rearrange("p (h x) -> p h x", h=4), ckvT[:, c, :], rhs,
                             start=(c == 0), stop=(c == 1), r=[wk, "ckvT", BA[half]], w=[("A", half)])
                for half in range(2):
                    s.cp("act", big[1][:, half * 4:(half + 1) * 4, :], A[half][:].rearrange("p (h t) -> p h t", h=4),
                         r=[("A", half), BA[half]], w=[("big", 1)])
                s.dma("sp", o_v[:, tok, :].rearrange("h t x -> t h x"), big[1][:], r=[("big", 1)], slot="s_v")
                wv, wk = wload(w_uq.rearrange("(c p) n -> p c n", p=128),
                               lambda b: b[:].rearrange("p c n -> p (c n)")[:, 0:4608].rearrange("p (c n) -> p c n", c=3))
                for h in range(8):
                    half, h4 = h // 4, h % 4
                    for c in range(3):
                        s.mm(A[half][:, h4 * 128:(h4 + 1) * 128], wv[:, c, h * 192:h * 192 + 128], cqT[:, c, :],
                             start=(c == 0), stop=(c == 2), r=[wk, "cqT", BA[half]], w=[("A", half)])
                for half in range(2):
                    s.cp("act", big[0][:, half * 4:(half + 1) * 4, :], A[half][:].rearrange("p (h t) -> p h t", h=4),
                         r=[("A", half), BA[half]], w=[("big", 0)])
                s.dma("sp", o_qn[:, :, tok].rearrange("h d t -> d h t"), big[0][:], r=[("big", 0)], slot="s_qn")
                for c in range(3):
                    rhs = wv[:, c, :].rearrange("p (h x) -> p h x", x=192)[:, :, 128:192]
                    s.mm(Sb[0][:].rearrange("p (h x) -> p h x", h=8), cqT[:, c, :], rhs, start=(c == 0), stop=(c == 2),
                         r=[wk, "cqT", BSb[0]], w=[("sbk", 0)])
                qv = Sb[0][:].rearrange("p (h a x) -> p h a x", h=8, a=2)
                qo = qro[:].rearrange("p (h a x) -> p h a x", h=8, a=2)
                csb = cs[:].unsqueeze(1).to_broadcast([128, 8, 32])
                snb = sn[:].unsqueeze(1).to_broadcast([128, 8, 32])
                t256v = t256[:].rearrange("p (h x) -> p h x", h=8)
                s.tt("dve", qo[:, :, 0, :], qv[:, :, 0, :], csb, ALU.mult, r=[("sbk", 0), BSb[0], "cs"], w=["qro"])
                s.tt("dve", t256v, qv[:, :, 1, :], snb, ALU.mult, r=[("sbk", 0), BSb[0], "sn"], w=["t256"])
                s.tt("dve", qo[:, :, 0, :], qo[:, :, 0, :], t256v, ALU.subtract, r=["qro", "t256"], w=["qro"])
                s.tt("dve", qo[:, :, 1, :], qv[:, :, 0, :], snb, ALU.mult, r=[("sbk", 0), BSb[0], "sn", "qro"], w=["qro"])
                s.tt("dve", t256v, qv[:, :, 1, :], csb, ALU.mult, r=[("sbk", 0), BSb[0], "cs", "qro"], w=["t256"])
                s.tt("dve", qo[:, :, 1, :], qo[:, :, 1, :], t256v, ALU.add, r=["qro", "t256"], w=["qro"])
                for h in range(8):
                    hb, h4 = h // 4, h % 4
                    s.tr(Tb[hb][0:64, h4 * 128:(h4 + 1) * 128], qro[:, h * 64:(h + 1) * 64], ident,
                         r=["qro", "cst", BTb[hb]], w=[("tb", hb)])
                for hb in range(2):
                    s.cp("act", big[1][0:64, hb * 4:(hb + 1) * 4, :], Tb[hb][0:64, :].rearrange("p (h t) -> p h t", h=4),
                         r=[("tb", hb), BTb[hb]], w=[("big", 1)])
                s.dma("sp", o_qr[:, :, tok].rearrange("h d t -> d h t"), big[1][0:64, :, :], r=[("big", 1)], slot="s_qr")
    return


def build_T(NT, kind, limit_slots=128):
    nc = bass.Bass("TRN2", target_bir_lowering=False)
    NTOK = NT * 128
    t = t_decl(nc, NT, kind)
    t.update(xin=din(nc, "xin", [NTOK, D]), mT=din(nc, "mT", [D, NTOK]), cst_t=din(nc, "cst", [128, 416]))
    if kind == "A":
        t.update(o_x2=dout(nc, "x2", [NTOK, D]), o_qn=dout(nc, "qnT", [8, 128, NTOK]), o_qr=dout(nc, "qrT", [8, 64, NTOK]),
                 o_kn=dout(nc, "knT", [8, 128, NTOK]), o_kr=dout(nc, "krT", [64, NTOK]), o_v=dout(nc, "v", [8, NTOK, 128]))
    else:
        t.update(o_x2=dout(nc, "out", [NTOK, D]))
    with ExitStack() as st:
        s = Sched(nc, st)
        phase_T(s, NT, kind, t, limit_slots)
        n = s.emit()
    return nc, n


def t_inputs(inp, layer, kind, xin, m, tok, mT=None, tables=True):
    f = np.ascontiguousarray
    if mT is None:
        mT = m.T
    lnp = np.stack([inp["ln_g"][layer, 0], inp["ln_b"][layer, 0], inp["ln_g"][layer, 1], inp["ln_b"][layer, 1]], 0)
    d = dict(
        xin=f(xin), mT=f(mT),
        w_out=f(inp["a_w_out"][0] if kind == "A" else inp["b_w_out"][0]),
        lnp=f(np.broadcast_to(lnp[None], (128, 4, D))),
        w_q=f(inp["peer_w_q"][layer]),
        skT=f(inp["peer_sub_keys"][layer].reshape(16, 128, 128).transpose(2, 0, 1)),
        peer_u=f(inp["peer_u"][layer]) if tables else None, peer_v=f(inp["peer_v"][layer]) if tables else None,
        pT=f(inp["p"][layer, 0, tok].T), w_proj=f(inp["ple_w_proj"][layer]), w_gate=f(inp["ple_w_gate"][layer]),
        cst=t_consts(),
    )
    if kind == "A":
        d.update(
            w_dn=f(np.concatenate([inp["kv_w_down"], inp["b_w_dq"][0]], 1)),
            w_up=f(inp["kv_w_up"]), w_uq=f(inp["b_w_uq"][0]),
            gn=f(np.broadcast_to(np.concatenate([inp["kv_norm_g"], inp["b_q_norm_g"][0]])[None], (128, 640))),
            pos=f(inp["positions"][0, tok].reshape(-1, 1).astype(np.int32)),
        )
    return d


def at_consts(QB):
    k = np.arange(128)[:, None]
    q = np.arange(QB)[None, :]
    m = np.stack([(q >= jj * 128 + k) for jj in range(QB // 128)], 1).astype(np.float32)
    return np.ascontiguousarray(m)


def phase_AT(s, S, t):
    nc = s.nc
    QB = min(512, S)
    NJ = QB // 128
    NQ = S // QB
    NKT = S // 128
    msk, oT = t["msk"], t["oT"]
    fused = "qn_rows" in t
    if not fused:
        qnT, qrT, knT, krT, v = (t[k] for k in ("qnT", "qrT", "knT", "krT", "v"))
    SCALE = float(192.0 ** -0.5)
    s.phase()
    if True:
        K_bf = s.sb("Kbf", [128, S], BF16)
        KR_bf = s.sb("KRbf", [128, S], BF16)
        V_bf = s.sb("Vbf", [128, NKT, 128], BF16)
        m_bf = s.sb("mbf", [128, NJ, QB], BF16)
        ones = s.sb("ones", [128, 128])
        qn_bf = [s.sb(f"qn{j}", [128, QB], BF16) for j in range(2)]
        qr_bf = [s.sb(f"qr{j}", [128, QB], BF16) for j in range(2)]
        NP = 4
        PT = [s.sb(f"PT{j}", [128, QB], BF16) for j in range(NP)]
        lacc = [s.sb(f"lacc{j}", [128, QB]) for j in range(2)]
        rl = s.sb("rl", [128, QB])
        osb = [s.sb(f"osb{j}", [128, QB]) for j in range(2)]
        Sp = [s.ps(f"Sp{j}", [128, 512]) for j in range(2)]
        Op = [s.ps(f"Op{j}", [128, 512]) for j in range(2)]
        Lp = s.ps("Lp", [128, 512])
        BSp = [("B", "S0"), ("B", "S1")]
        BOp = [("B", "O0"), ("B", "O1")]
        BL = ("B", "L")
        s.memset("dve", ones[:], 1.0, w=["ones"])
        if not fused:
            CH = 2048
            for c in range(0, S, CH):
                w_ = min(CH, S - c)
                s.dma("pool", K_bf[:, c:c + w_], knT[:, c:c + w_], w=[("K", c // 128 + i) for i in range(w_ // 128)], slot="l_k")
                s.dma("pool", KR_bf[0:64, c:c + w_], krT[:, c:c + w_], w=[("KR", c // 128 + i) for i in range(w_ // 128)], slot="l_kr")
            vv = v.rearrange("(n p) d -> p n d", p=128)
            for c in range(0, NKT, 8):
                w_ = min(8, NKT - c)
                s.dma("pool", V_bf[:, c:c + w_, :], vv[:, c:c + w_, :], w=[("V", c + i) for i in range(w_)], slot="l_v")
        else:
            NTOK = S // NCORES
            nb = NTOK // 512
            ntl = NTOK // 128
            idxk_sb = s.sb("idxk", [128, 8 * nb], I32)
            idxv_sb = s.sb("idxv", [128, 8 * ntl], I32)
            s.dma("sp", idxk_sb[:], t["idxk"], w=["idxk"], slot="l_ik")
            s.dma("sp", idxv_sb[:], t["idxv"], w=["idxv"], slot="l_iv")
            for r_ in range(8):
                for blk in range(nb):
                    gb = r_ * nb + blk
                    s.gather(K_bf[:, gb * 512:(gb + 1) * 512], t["kn_rows"], idxk_sb[:, gb:gb + 1], r=["idxk"],
                             w=[("K", gb * 4 + i) for i in range(4)], slot=f"g_k{gb % 4}")
                for c in range(0, NTOK, 2048):
                    w_ = min(2048, NTOK - c)
                    s.dma("pool", KR_bf[0:64, r_ * NTOK + c:r_ * NTOK + c + w_], t["kr_all"][r_ * 64:(r_ + 1) * 64, c:c + w_],
                          w=[("KR", (r_ * NTOK + c) // 128 + i) for i in range(w_ // 128)], slot="l_kr")
                for tl in range(ntl):
                    kt = r_ * ntl + tl
                    s.gather(V_bf[:, kt, :], t["v_rows"], idxv_sb[:, kt:kt + 1], r=["idxv"], w=[("V", kt)],
                             slot=f"g_v{kt % 4}")
        s.dma("pool", m_bf[:], msk, w=["msk"], slot="l_m")
        for qb in range(NQ):
            qj = qb % 2
            qc = slice(qb * QB, (qb + 1) * QB)
            if not fused:
                s.dma("pool", qn_bf[qj][:], qnT[:, qc], w=[("qn", qj)], slot=f"l_qn{qj}")
                s.dma("pool", qr_bf[qj][0:64, :], qrT[:, qc], w=[("qr", qj)], slot=f"l_qr{qj}")
            else:
                s.gather(qn_bf[qj][:], t["qn_rows"], idxk_sb[:, qb:qb + 1], r=["idxk"], w=[("qn", qj)], slot=f"l_qn{qj}")
                s.gather(qr_bf[qj][:], t["qr_rows"], idxk_sb[:, qb:qb + 1], r=["idxk"], w=[("qr", qj)], slot=f"l_qr{qj}")
            nkb = (qb + 1) * NJ

            def pv(kb, qj=qj, nkb=nkb):
                s.mm(Op[qj][:, 0:QB], V_bf[:, kb, :], PT[kb % NP][:], start=(kb == 0), stop=(kb == nkb - 1),
                     r=[("V", kb), ("PT", kb % NP), BOp[qj]], w=[("Op", qj)])
            for kb in range(nkb):
                sj = kb % 2
                pj = kb % NP
                kc = slice(kb * 128, (kb + 1) * 128)
                s.mm(Sp[sj][:, 0:QB], K_bf[:, kc], qn_bf[qj][:], start=True, stop=False,
                     r=[("K", kb), ("qn", qj), BSp[sj]], w=[("Sp", sj)])
                s.mm(Sp[sj][:, 0:QB], KR_bf[0:64, kc], qr_bf[qj][0:64, :], start=False, stop=True,
                     r=[("KR", kb), ("qr", qj), BSp[sj]], w=[("Sp", sj)])
                s.act(PT[pj][:], Sp[sj][:, 0:QB], AF.Exp, scale=SCALE, r=[("Sp", sj), BSp[sj]], w=[("PT", pj)])
                jj = kb - qb * NJ
                if jj >= 0:
                    s.tt("dve", PT[pj][:], PT[pj][:], m_bf[:, jj, :], ALU.mult, r=[("PT", pj), "msk"], w=[("PT", pj)])
                if kb == 0:
                    s.cp("dve", lacc[qj][:], PT[pj][:], r=[("PT", pj)], w=[("lacc", qj)])
                else:
                    s.tt("dve", lacc[qj][:], lacc[qj][:], PT[pj][:], ALU.add, r=[("PT", pj), ("lacc", qj)], w=[("lacc", qj)])
                if kb >= 1:
                    pv(kb - 1)
            pv(nkb - 1)
            s.mm(Lp[:, 0:QB], ones[:], lacc[qj][:], r=["ones", ("lacc", qj), BL], w=["Lp"])
            s.recip(rl[:], Lp[:, 0:QB], r=["Lp", BL], w=["rl"])
            s.tt("dve", osb[qj][:], Op[qj][:, 0:QB], rl[:], ALU.mult, r=[("Op", qj), BOp[qj], "rl"], w=[("osb", qj)])
            s.dma("sp", oT[:, qc], osb[qj][:], r=[("osb", qj)], slot=f"s_o{qj}")
    return


def build_AT(S):
    nc = bass.Bass("TRN2", target_bir_lowering=False)
    QB = min(512, S)
    t = dict(qnT=din(nc, "qnT", [128, S]), qrT=din(nc, "qrT", [64, S]), knT=din(nc, "knT", [128, S]),
             krT=din(nc, "krT", [64, S]), v=din(nc, "v", [S, 128]), msk=din(nc, "msk", [128, QB // 128, QB]),
             oT=dout(nc, "oT", [128, S]))
    with ExitStack() as st:
        s = Sched(nc, st)
        phase_AT(s, S, t)
        n = s.emit()
    return nc, n


def build_fused(S, limit_slots=128):
    nc = bass.Bass("TRN2", target_bir_lowering=False)
    NT = S // 128 // NCORES
    NTOK = NT * 128
    QB = 512
    nb = NTOK // 512
    xT_sh = din(nc, "xT_sh", [128, S])
    tab_sh = [din(nc, f"tab_sh{k}", [2048, D]) for k in range(4)]
    tM = dict(wqk=din(nc, "wqk", [D, 128]), wtm=din(nc, "wtm", [D, 322]),
              bif=din(nc, "bif", [128, 2]), hng=din(nc, "hng", [128, 128]), cst_m=din(nc, "cst_m", [128, M_CST]),
              ident=din(nc, "ident", [128, 128]))
    tA = t_decl(nc, NT, "A", "_0", tables=False)
    tC = t_decl(nc, NT, "C", "_1", tables=False)
    cst_t = din(nc, "cst_t", [128, 416])
    xin = din(nc, "xin", [NTOK, D])
    msk = din(nc, "msk", [128, QB // 128, QB])
    idxm = din(nc, "idxm", [128, NT * 8], I32)
    idxk = din(nc, "idxk", [128, 8 * nb], I32)
    idxv = din(nc, "idxv", [128, 8 * NT], I32)
    out = dout(nc, "out", [NTOK, D])

    def dr(name, shape):
        return nc.dram_tensor(name, list(shape), F32).ap()
    g_src = dr("g_src", [128, S]); g_all = dr("g_all", [8 * 128, S])
    o_src = dr("o_src", [128, S]); o_all = dr("o_all", [8 * 128, S])
    x2buf = dr("x2buf", [NTOK, D])
    qn_src = dr("qn_src", [8, 128, NTOK]); qn_all = dr("qn_all", [64 * 128, NTOK])
    qr_src = dr("qr_src", [8, 128, NTOK]); qr_all = dr("qr_all", [64 * 128, NTOK])
    kn_src = dr("kn_src", [8, 128, NTOK]); kn_all = dr("kn_all", [64 * 128, NTOK])
    kr_src = dr("kr_src", [64, NTOK]); kr_all = dr("kr_all", [8 * 64, NTOK])
    v_src = dr("v_src", [8, NTOK, 128]); v_all = dr("v_all", [64 * NTOK, 128])
    xT_src = dr("xT_src", [128, S]); xT_full = dr("xT_full", [D, S])
    tab_src = [dr(f"tab_src{k}", [2048, D]) for k in range(4)]
    tab_full = [dr(f"tab_full{k}", [16384, D]) for k in range(4)]
    tA["pu"], tA["pv"], tC["pu"], tC["pv"] = tab_full
    with ExitStack() as st:
        s = Sched(nc, st)
        s.dma("sp", xT_src, xT_sh, w=["xT_src"], slot="c_x")
        s.cc("AllGather", xT_src, xT_full, r=["xT_src"], w=["xT_full"], slot="cc_x")

        def tables():
            for k in range(4):
                s.dma("sp", tab_src[k], tab_sh[k], w=[("tab_src", k)], slot=f"c_t{k}")
                s.cc("AllGather", tab_src[k], tab_full[k], r=[("tab_src", k)], w=[("tab_full", k)], slot=f"cc_t{k}")
        tM["gT"] = g_src
        tM["xT"] = xT_full
        phase_M(s, S, tM, after_fence=tables)
        s.phase()
        s.cc("AllGather", g_src, g_all, slot="cc_g")
        tA.update(xin=xin, cst_t=cst_t, g_rows=g_all.rearrange("r (b t) -> (r b) t", t=128), idxm=idxm,
                  o_x2=x2buf, o_qn=qn_src, o_qr=qr_src[:, 0:64, :], o_kn=kn_src, o_kr=kr_src, o_v=v_src)
        phase_T(s, NT, "A", tA, limit_slots)
        s.phase()
        s.cc("AllGather", qn_src.rearrange("h d t -> (h d) t"), qn_all, slot="cc_qn")
        s.cc("AllGather", qr_src.rearrange("h d t -> (h d) t"), qr_all, slot="cc_qr")
        s.cc("AllGather", kn_src.rearrange("h d t -> (h d) t"), kn_all, slot="cc_kn")
        s.cc("AllGather", kr_src, kr_all, slot="cc_kr")
        s.cc("AllGather", v_src.rearrange("h t x -> (h t) x"), v_all, slot="cc_v")
        tAT = dict(msk=msk, oT=o_src, idxk=idxk, idxv=idxv,
                   qn_rows=qn_all.rearrange("x (b t) -> (x b) t", t=512), qr_rows=qr_all.rearrange("x (b t) -> (x b) t", t=512),
                   kn_rows=kn_all.rearrange("x (b t) -> (x b) t", t=512), kr_all=kr_all, v_rows=v_all)
        phase_AT(s, S, tAT)
        s.phase()
        s.cc("AllGather", o_src, o_all, slot="cc_o")
        tC.update(xin=x2buf, cst_t=cst_t, g_rows=o_all.rearrange("r (b t) -> (r b) t", t=128), idxm=idxm, o_x2=out)
        phase_T(s, NT, "C", tC, limit_slots)
        n = s.emit()
    return nc, n


def fused_inputs(inp, S):
    f = np.ascontiguousarray
    NT = S // 128 // NCORES
    NTOK = NT * 128
    NBLK = S // 128
    nb = NTOK // 512
    mm_ = m_inputs(inp, S)
    msk = at_consts(512)
    cst_t = t_consts()
    p_ = np.arange(128)
    maps = []
    for c in range(NCORES):
        tok = slice(c * NTOK, (c + 1) * NTOK)
        d = dict(mm_[c])
        d["cst_m"] = d.pop("cst")
        a = t_inputs(inp, 0, "A", inp["x"][0, tok], None, tok, mT=np.zeros((1, 1), np.float32), tables=False)
        b = t_inputs(inp, 1, "C", inp["x"][0, tok], None, tok, mT=np.zeros((1, 1), np.float32), tables=False)
        for k in ("w_out", "lnp", "w_q", "skT", "pT", "w_proj", "w_gate"):
            d[k + "_0"] = a[k]
            d[k + "_1"] = b[k]
        rows = slice(c * 2048, (c + 1) * 2048)
        d["tab_sh0"] = f(inp["peer_u"][0][rows]); d["tab_sh1"] = f(inp["peer_v"][0][rows])
        d["tab_sh2"] = f(inp["peer_u"][1][rows]); d["tab_sh3"] = f(inp["peer_v"][1][rows])
        d["xT_sh"] = f(d.pop("xT")[c * 128:(c + 1) * 128])
        for k in ("w_dn", "w_up", "w_uq", "gn", "pos"):
            d[k] = a[k]
        d["cst_t"] = cst_t
        d["xin"] = f(inp["x"][0, tok])
        d["msk"] = msk
        h = c
        idxm = np.zeros((128, NT * 8), np.int32)
        for t in range(NT):
            for hh in range(8):
                idxm[:, t * 8 + hh] = (hh * 128 + p_) * NBLK + (c * NT + t)
        idxk = np.zeros((128, 8 * nb), np.int32)
        for r in range(8):
            for blk in range(nb):
                idxk[:, r * nb + blk] = ((r * 8 + h) * 128 + p_) * nb + blk
        idxv = np.zeros((128, 8 * NT), np.int32)
        for r in range(8):
            for tl in range(NT):
                idxv[:, r * NT + tl] = (r * 8 + h) * NTOK + tl * 128 + p_
        d["idxm"], d["idxk"], d["idxv"] = idxm, idxk, idxv
        maps.append(d)
    return maps


def m_inputs(inp, S):
    f = np.ascontiguousarray
    xT = f(inp["x"][0, :S].T)
    w_in = inp["a_w_in"][0]
    cst = m_consts()
    ident = np.eye(128, dtype=np.float32)
    maps = []
    for h in range(8):
        q = w_in[:, h * 64:(h + 1) * 64]
        k = w_in[:, 512 + h * 64:512 + (h + 1) * 64]
        v = w_in[:, 1024 + h * 128:1024 + (h + 1) * 128]
        gi = w_in[:, 2048 + h:2049 + h]
        gf = w_in[:, 2056 + h:2057 + h]
        og = w_in[:, 2064 + h * 128:2064 + (h + 1) * 128]
        maps.append(dict(
            xT=xT, wqk=f(np.concatenate([q, k], 1)), wtm=f(np.concatenate([v, og, k, gi, gf], 1)),
            bif=f(np.broadcast_to(inp["a_b_if"][0][:, h][None, :], (128, 2))),
            hng=f(np.broadcast_to(inp["a_hn_g"][0][h * 128:(h + 1) * 128][None, :], (128, 128))),
            cst=cst, ident=ident))
    return maps


_PROGS = {}


def _prog(key, fn):
    if key not in _PROGS:
        _PROGS[key] = fn()[0]
    return _PROGS[key]


def kernel(**inp):
    inp = {k: np.asarray(v) for k, v in inp.items()}
    S = inp["x"].shape[1]
    res = run_bass_kernel_spmd(_prog(("F", S), lambda: build_fused(S)), fused_inputs(inp, S), core_ids=list(range(NCORES)))
    out = np.concatenate([r["out"] for r in res.results], 0)
    return out.reshape(1, S, D).astype(np.float32)
```
